# Optimizing a Trainium2 kernel written in Bass

```python
import math
import jax, jax.numpy as jnp
from jax import lax
import numpy as np

D_MODEL = 1024
BATCH = 8
SEQ = 4096
DEPTH = 4

CHUNK = 128
CONV_K = 4
EPS = 1e-6
ML_HEADS = 4
ML_DV = D_MODEL // ML_HEADS
ML_DK = ML_DV // 2
ML_WIDTH = ML_HEADS * ML_DV
ML_QK = ML_HEADS * ML_DK
SSM_WIDTH = D_MODEL
SSM_HEADDIM = 64
SSM_HEADS = SSM_WIDTH // SSM_HEADDIM
SSM_GROUPS = 2
SSM_STATE = 128
SSM_CONV = SSM_WIDTH + 2 * SSM_GROUPS * SSM_STATE
D_MIX = ML_WIDTH + SSM_WIDTH
D_FF = -(-8 * D_MODEL // (3 * 256)) * 256
IN_SIZES = (ML_QK, ML_QK, ML_WIDTH, ML_WIDTH, ML_HEADS, ML_HEADS, SSM_WIDTH, SSM_CONV, SSM_HEADS)
D_IN = ML_QK * 2 + ML_WIDTH * 2 + ML_HEADS * 2 + SSM_WIDTH + SSM_CONV + SSM_HEADS

kernel_name = "hymba_mlstm_mamba2_adaln_trunk"


def rms_norm(x, w):
    xf = x.astype(jnp.float32)
    out = xf * lax.rsqrt(jnp.mean(xf * xf, axis=-1, keepdims=True) + EPS)
    return (out * w.astype(jnp.float32)).astype(x.dtype)


def split_sizes(t, sizes):
    offsets, acc = [], 0
    for s in sizes[:-1]:
        acc += s
        offsets.append(acc)
    return jnp.split(t, offsets, axis=-1)


def causal_dwconv(u, w, b):
    k = w.shape[0]
    out = lax.conv_general_dilated(
        u, w[:, None, :].astype(u.dtype), window_strides=(1,), padding=[(k - 1, 0)],
        dimension_numbers=("NWC", "WIO", "NWC"), feature_group_count=u.shape[-1])
    return out + b.astype(u.dtype)


def mlstm_chunkwise(q, k, v, i_pre, f_pre):
    f32 = jnp.float32
    bsz, seq, nh, dk = q.shape
    dv = v.shape[-1]
    nc = seq // CHUNK

    def c4(t):
        return t.astype(f32).reshape(bsz, nc, CHUNK, nh, t.shape[-1]).transpose(1, 0, 3, 2, 4)

    def c3(t):
        return t.astype(f32).reshape(bsz, nc, CHUNK, nh).transpose(1, 0, 3, 2)

    causal = jnp.tril(jnp.ones((CHUNK, CHUNK), dtype=bool))

    def step(carry, inp):
        cmat, nvec, m = carry
        qc, kc, vc, ic, lfc = inp
        b = jnp.cumsum(lfc, axis=-1)
        dlog = b[..., :, None] - b[..., None, :] + ic[..., None, :]
        dlog = jnp.where(causal, dlog, -jnp.inf)
        inter = b + m[..., None]
        m_t = jnp.maximum(inter, jnp.max(dlog, axis=-1))
        wts = jnp.exp(dlog - m_t[..., None])
        scores = jnp.einsum("bhtd,bhsd->bhts", qc, kc) * wts
        w_inter = jnp.exp(inter - m_t)
        num = jnp.einsum("bhts,bhsv->bhtv", scores, vc) + \
            w_inter[..., None] * jnp.einsum("bhtd,bhdv->bhtv", qc, cmat)
        den = jnp.sum(scores, axis=-1) + w_inter * jnp.einsum("bhtd,bhd->bht", qc, nvec)
        h = num / jnp.maximum(jnp.abs(den), jnp.exp(-m_t))[..., None]
        b_last = b[..., -1]
        g = b_last[..., None] - b + ic
        m_new = jnp.maximum(b_last + m, jnp.max(g, axis=-1))
        decay = jnp.exp(b_last + m - m_new)
        ws = jnp.exp(g - m_new[..., None])
        cmat = decay[..., None, None] * cmat + jnp.einsum("bhs,bhsd,bhsv->bhdv", ws, kc, vc)
        nvec = decay[..., None] * nvec + jnp.einsum("bhs,bhsd->bhd", ws, kc)
        return (cmat, nvec, m_new), h

    init = (jnp.zeros((bsz, nh, dk, dv), f32), jnp.zeros((bsz, nh, dk), f32),
            jnp.zeros((bsz, nh), f32))
    log_f = jax.nn.log_sigmoid(f_pre.astype(f32))
    _, h = lax.scan(step, init, (c4(q), c4(k), c4(v), c3(i_pre), c3(log_f)))
    return h.transpose(1, 0, 3, 2, 4).reshape(bsz, seq, nh, dv)


def ssd_chunked(x, dt, a_neg, bm, cm):
    f32 = jnp.float32
    bsz, seq, nh, hp = x.shape
    ng, ns = bm.shape[-2:]
    hg = nh // ng
    nc = seq // CHUNK
    xc = x.astype(f32).reshape(bsz, nc, CHUNK, ng, hg, hp).transpose(1, 0, 3, 4, 2, 5)
    dtc = dt.reshape(bsz, nc, CHUNK, ng, hg).transpose(1, 0, 3, 4, 2)
    ac = (dt * a_neg).reshape(bsz, nc, CHUNK, ng, hg).transpose(1, 0, 3, 4, 2)
    bc = bm.astype(f32).reshape(bsz, nc, CHUNK, ng, ns).transpose(1, 0, 3, 2, 4)
    cc = cm.astype(f32).reshape(bsz, nc, CHUNK, ng, ns).transpose(1, 0, 3, 2, 4)
    causal = jnp.tril(jnp.ones((CHUNK, CHUNK), dtype=bool))

    def step(state, inp):
        xk, dtk, ak, bk, ck = inp
        acum = jnp.cumsum(ak, axis=-1)
        seg = acum[..., :, None] - acum[..., None, :]
        decay = jnp.exp(jnp.where(causal, seg, -jnp.inf))
        cb = jnp.einsum("bgtn,bgsn->bgts", ck, bk)
        mix = cb[:, :, None] * decay * dtk[..., None, :]
        y = jnp.einsum("bghts,bghsp->bghtp", mix, xk)
        y = y + jnp.exp(acum)[..., None] * jnp.einsum("bgtn,bghpn->bghtp", ck, state)
        w_s = jnp.exp(acum[..., -1:] - acum) * dtk
        state = jnp.exp(acum[..., -1])[..., None, None] * state + \
            jnp.einsum("bghs,bghsp,bgsn->bghpn", w_s, xk, bk)
        return state, y

    init = jnp.zeros((bsz, ng, hg, hp, ns), f32)
    _, y = lax.scan(step, init, (xc, dtc, ac, bc, cc))
    return y.transpose(1, 0, 4, 2, 3, 5).reshape(bsz, seq, nh, hp)


def mlstm_group(q, k, v, o_pre, i_pre, f_pre, conv_w, conv_b, i_b, f_b, norm_w):
    bsz, seq, _ = q.shape
    qk = jax.nn.silu(causal_dwconv(jnp.concatenate([q, k], axis=-1), conv_w, conv_b))
    q, k = jnp.split(qk, 2, axis=-1)
    q = q.reshape(bsz, seq, ML_HEADS, ML_DK) * (ML_DK ** -0.5)
    k = k.reshape(bsz, seq, ML_HEADS, ML_DK)
    v = v.reshape(bsz, seq, ML_HEADS, ML_DV)
    h = mlstm_chunkwise(q, k, v, i_pre + i_b, f_pre + f_b)
    h = h * lax.rsqrt(jnp.mean(h * h, axis=-1, keepdims=True) + EPS)
    h = h.reshape(bsz, seq, ML_WIDTH) * norm_w.astype(jnp.float32)
    return (jax.nn.sigmoid(o_pre.astype(jnp.float32)) * h).astype(o_pre.dtype)


def mamba2_group(z, xbc, dt_pre, conv_w, conv_b, dt_b, a_log, d_skip, norm_w):
    bsz, seq, _ = z.shape
    xbc = jax.nn.silu(causal_dwconv(xbc, conv_w, conv_b))
    xs, bs, cs = split_sizes(xbc, (SSM_WIDTH, SSM_GROUPS * SSM_STATE, SSM_GROUPS * SSM_STATE))
    xs = xs.reshape(bsz, seq, SSM_HEADS, SSM_HEADDIM).astype(jnp.float32)
    bs = bs.reshape(bsz, seq, SSM_GROUPS, SSM_STATE)
    cs = cs.reshape(bsz, seq, SSM_GROUPS, SSM_STATE)
    dt = jax.nn.softplus((dt_pre + dt_b).astype(jnp.float32))
    a_neg = -jnp.exp(a_log.astype(jnp.float32))
    y = ssd_chunked(xs, dt, a_neg, bs, cs) + d_skip.astype(jnp.float32)[:, None] * xs
    y = y.reshape(bsz, seq, SSM_WIDTH) * jax.nn.silu(z.astype(jnp.float32))
    return rms_norm(y, norm_w).astype(z.dtype)


def setup_inputs(seed: int = 0) -> dict:
    key = jax.random.key(seed)
    ks = jax.random.split(key, 24)
    f32 = jnp.float32

    def nrm(k, shape, scale):
        return jax.random.normal(k, shape, f32) * scale

    dt0 = jnp.exp(jax.random.uniform(ks[14], (DEPTH, SSM_HEADS), f32,
                                     minval=math.log(1e-3), maxval=math.log(1e-1)))
    return {
        "x": nrm(ks[0], (BATCH, SEQ, D_MODEL), 1.0),
        "c": nrm(ks[1], (BATCH, D_MODEL), 1.0),
        "w_ada": nrm(ks[2], (DEPTH, D_MODEL, 6 * D_MODEL), D_MODEL ** -0.5),
        "b_ada": nrm(ks[3], (DEPTH, 6 * D_MODEL), 0.02),
        "norm1_w": 1.0 + nrm(ks[4], (DEPTH, D_MODEL), 0.02),
        "w_in": nrm(ks[5], (DEPTH, D_MODEL, D_IN), D_MODEL ** -0.5),
        "conv_qk_w": nrm(ks[6], (DEPTH, CONV_K, 2 * ML_QK), CONV_K ** -0.5),
        "conv_qk_b": nrm(ks[7], (DEPTH, 2 * ML_QK), 0.02),
        "i_bias": nrm(ks[8], (DEPTH, ML_HEADS), 0.1),
        "f_bias": jnp.linspace(3.0, 6.0, ML_HEADS, dtype=f32)[None, :] + nrm(ks[9], (DEPTH, ML_HEADS), 0.1),
        "mlstm_norm_w": 1.0 + nrm(ks[10], (DEPTH, ML_WIDTH), 0.02),
        "conv_ssm_w": nrm(ks[11], (DEPTH, CONV_K, SSM_CONV), CONV_K ** -0.5),
        "conv_ssm_b": nrm(ks[12], (DEPTH, SSM_CONV), 0.02),
        "dt_bias": dt0 + jnp.log(-jnp.expm1(-dt0)),
        "a_log": jnp.log(jax.random.uniform(ks[13], (DEPTH, SSM_HEADS), f32, minval=1.0, maxval=16.0)),
        "d_skip": 1.0 + nrm(ks[15], (DEPTH, SSM_HEADS), 0.02),
        "ssm_norm_w": 1.0 + nrm(ks[16], (DEPTH, SSM_WIDTH), 0.02),
        "w_out": nrm(ks[17], (DEPTH, D_MIX, D_MODEL), D_MIX ** -0.5),
        "norm2_w": 1.0 + nrm(ks[18], (DEPTH, D_MODEL), 0.02),
        "w_ffn_in": nrm(ks[19], (DEPTH, D_MODEL, 2 * D_FF), D_MODEL ** -0.5),
        "w_ffn_out": nrm(ks[20], (DEPTH, D_FF, D_MODEL), D_FF ** -0.5),
        "final_norm_w": 1.0 + nrm(ks[21], (D_MODEL,), 0.02),
    }


def reference(x, c, w_ada, b_ada, norm1_w, w_in, conv_qk_w, conv_qk_b, i_bias, f_bias,
              mlstm_norm_w, conv_ssm_w, conv_ssm_b, dt_bias, a_log, d_skip, ssm_norm_w,
              w_out, norm2_w, w_ffn_in, w_ffn_out, final_norm_w):
    cond = jax.nn.silu(c)
    for l in range(DEPTH):
        mod = cond @ w_ada[l] + b_ada[l]
        sh1, sc1, g1, sh2, sc2, g2 = [m[:, None, :] for m in jnp.split(mod, 6, axis=-1)]
        h = rms_norm(x, norm1_w[l]) * (1.0 + sc1) + sh1
        proj = h @ w_in[l]
        q, k, v, o_pre, i_pre, f_pre, z, xbc, dt_pre = split_sizes(proj, IN_SIZES)
        y_ml = mlstm_group(q, k, v, o_pre, i_pre, f_pre, conv_qk_w[l], conv_qk_b[l],
                           i_bias[l], f_bias[l], mlstm_norm_w[l])
        y_ssm = mamba2_group(z, xbc, dt_pre, conv_ssm_w[l], conv_ssm_b[l], dt_bias[l],
                             a_log[l], d_skip[l], ssm_norm_w[l])
        y = jnp.concatenate([y_ml.astype(x.dtype), y_ssm.astype(x.dtype)], axis=-1) @ w_out[l]
        x = x + g1 * y
        h = rms_norm(x, norm2_w[l]) * (1.0 + sc2) + sh2
        gate, up = jnp.split(h @ w_ffn_in[l], 2, axis=-1)
        x = x + g2 * ((jax.nn.silu(gate) * up) @ w_ffn_out[l])
    return rms_norm(x, final_norm_w)
```

```python
import math
import numpy as np
import concourse.bass as bass
import concourse.mybir as mybir
from concourse.bass_utils import run_bass_kernel_spmd

F32 = mybir.dt.float32
BF16 = mybir.dt.bfloat16
ALU = mybir.AluOpType
AF = mybir.ActivationFunctionType

D = 1024
DIN = 5656
DFF = 2816
EPS = 1e-6
NEG_BIG = -30000.0


class Buf:
    __slots__ = ("name", "w", "r", "ld_sem", "ld_val", "st_sem", "st_val")

    def __init__(self, name):
        self.name = name
        self.w = None
        self.r = {}
        self.ld_sem = None
        self.ld_val = 0
        self.st_sem = None
        self.st_val = 0


class T:
    __slots__ = ("t", "b")

    def __init__(self, t, b):
        self.t = t
        self.b = b


def _b(x):
    return x.b if isinstance(x, T) else x


class _Rec:
    __slots__ = ("call",)

    def __init__(self):
        self.call = None

    def __getattr__(self, name):
        def f(*a, **k):
            self.call = (name, a, k)
            return self
        return f


class Eng:
    def __init__(self, name, handle, is_pe=False):
        self.name = name
        self.h = handle
        self.sem = None
        self.count = 0
        self.waited = {}
        self.ops = []
        self.is_pe = is_pe
        self.n_instr = 0


class FW:
    def __init__(self, nc):
        self.nc = nc
        self.sems = {}
        self.pe = Eng("pe", nc.tensor, True)
        self.act = Eng("act", nc.scalar)
        self.dve = Eng("dve", nc.vector)
        self.pool = Eng("pool", nc.gpsimd)
        self.sp = Eng("sp", nc.sync)
        self.engs = [self.pe, self.act, self.dve, self.pool, self.sp]
        self._ctx = []
        self._semctx = []
        self._uid = 0
        self.dma_toks = {}
        self.sem_pool = []
        self.sem_live = []
        for e in self.engs:
            e.sem = self.new_sem("e_" + e.name)

    def new_sem(self, name):
        cm = self.nc.semaphore(name)
        h = cm.__enter__()
        self._semctx.append(cm)
        self.sems[name] = h
        return name

    def tile(self, name, shape, dt):
        self._uid += 1
        name = "%s_%d" % (name, self._uid)
        cm = self.nc.sbuf_tensor(name, list(shape), dt)
        t = cm.__enter__()
        self._ctx.append(cm)
        return T(t, Buf(name))

    def psum(self, name, shape, dt):
        cm = self.nc.psum_tensor(name, list(shape), dt)
        t = cm.__enter__()
        self._ctx.append(cm)
        return T(t, Buf(name))

    def mark(self):
        return (len(self._ctx), len(self.sem_live))

    def release(self, mark):
        nctx, nsem = mark
        while len(self._ctx) > nctx:
            self._ctx.pop().__exit__(None, None, None)
        while len(self.sem_live) > nsem:
            k = self.sem_live.pop()
            self.sem_pool.append((k, self.dma_toks.get(k, 0)))

    def acquire_dma_sem(self, eng):
        tag = "sw" if eng is self.pool else "hw"
        for i, (k, v) in enumerate(self.sem_pool):
            if k.startswith(tag):
                self.sem_pool.pop(i)
                break
        else:
            self._uid += 1
            k, v = self.new_sem("%sdma%d" % (tag, self._uid)), 0
        self.sem_live.append(k)
        return k, v

    def close(self):
        self.release((0, 0))
        for cm in reversed(self._semctx):
            cm.__exit__(None, None, None)
        self._semctx = []

    def _need(self, eng, reads, writes, dma=False):
        need = {}

        def add(tok, skip_same):
            if tok is None:
                return
            k, v = tok
            if k == eng.sem and skip_same:
                return
            if v > need.get(k, 0):
                need[k] = v
        skip = eng.is_pe and not dma
        for b in reads:
            add(b.w, skip)
        for b in writes:
            add(b.w, skip)
            for k, v in b.r.items():
                add((k, v), skip)
        return need

    def _emit_waits(self, eng, need):
        for k, v in need.items():
            if eng.waited.get(k, 0) >= v:
                continue
            eng.waited[k] = v
            eng.ops.append(("wait", self.sems[k], v))
            eng.n_instr += 1

    def op(self, eng, fn, reads=(), writes=(), inc=True):
        reads = [_b(x) for x in reads]
        writes = [_b(x) for x in writes]
        need = self._need(eng, reads, writes)
        if need.get(eng.sem, 0) > eng.count:
            raise RuntimeError("self-dependency on pending instruction: " + eng.name)
        self._emit_waits(eng, need)
        val = eng.count + 1
        tok = (eng.sem, val)
        rec = _Rec()
        fn(rec)
        if inc:
            eng.ops.append(("op", rec.call, self.sems[eng.sem]))
            eng.count = val
        else:
            eng.ops.append(("op", rec.call, None))
        eng.n_instr += 1
        for b in reads:
            if b.r.get(tok[0], 0) < tok[1]:
                b.r[tok[0]] = tok[1]
        for b in writes:
            b.w = tok
            b.r = {}
        return tok

    def dma(self, eng, out_ap, in_ap, reads, writes, sem_buf, kind, **kw):
        reads = [_b(x) for x in reads]
        writes = [_b(x) for x in writes]
        sem_buf = _b(sem_buf)
        if kind == "ld":
            if sem_buf.ld_sem is None:
                sem_buf.ld_sem, sem_buf.ld_val = self.acquire_dma_sem(eng)
            sk = sem_buf.ld_sem
            sem_buf.ld_val += 16
            val = sem_buf.ld_val
        else:
            if sem_buf.st_sem is None:
                sem_buf.st_sem, sem_buf.st_val = self.acquire_dma_sem(eng)
            sk = sem_buf.st_sem
            sem_buf.st_val += 16
            val = sem_buf.st_val
        need = self._need(eng, reads, writes, dma=True)
        if need.get(eng.sem, 0) > eng.count:
            raise RuntimeError("dma self-dependency on pending instruction")
        self._emit_waits(eng, need)
        eng.ops.append(("dma", out_ap, in_ap, self.sems[sk], kw))
        eng.n_instr += 1
        tok = (sk, val)
        self.dma_toks[sk] = val
        for b in reads:
            if b.r.get(sk, 0) < val:
                b.r[sk] = val
        for b in writes:
            b.w = tok
            b.r = {}
        return tok

    def wait_all(self, eng, toks):
        need = {}
        for k, v in toks:
            if k != eng.sem:
                need[k] = max(need.get(k, 0), v)
        self._emit_waits(eng, need)

    def barrier(self):
        toks = [(e.sem, e.count) for e in self.engs if e.count > 0]
        toks += list(self.dma_toks.items())
        for e in self.engs:
            self.wait_all(e, toks)

    def replay(self):
        nc = self.nc
        with nc.Block() as block:
            def run(eng):
                def body(h):
                    for o in eng.ops:
                        if o[0] == "wait":
                            h.wait_ge(o[1], o[2])
                        elif o[0] == "op":
                            nm, a, k = o[1]
                            ins = getattr(h, nm)(*a, **k)
                            if o[2] is not None:
                                ins.then_inc(o[2], 1)
                        else:
                            _, out_ap, in_ap, sem, kw = o
                            h.dma_start(out=out_ap, in_=in_ap, **kw).then_inc(sem, 16)
                return body
            block.tensor(run(self.pe))
            block.scalar(run(self.act))
            block.vector(run(self.dve))
            block.gpsimd(run(self.pool))
            block.sync(run(self.sp))


def build(S=4096, depth=4, dbg=None):
    nch = S // 128
    nc = bass.Bass("TRN2", target_bir_lowering=False)

    def din(name, shape):
        return nc.dram_tensor(name, list(shape), F32, kind="ExternalInput").ap()

    x_in = din("x", [S, D])
    c_t = din("c_t", [128, 8])
    w_ada = din("w_ada", [depth, D, 6 * D])
    b_ada = din("b_ada", [depth, 6 * D])
    norm1_w = din("norm1_w", [depth, D])
    w_in = din("w_in", [depth, D, DIN])
    cqk = din("cqk", [depth, 128, 8, 5])
    small = din("small", [depth, 64])
    mlnw = din("mlstm_norm_w", [depth, D])
    cssm = din("cssm", [depth, 128, 12, 5])
    ssnw = din("ssm_norm_w", [depth, D])
    w_out = din("w_out", [depth, 2 * D, D])
    norm2_w = din("norm2_w", [depth, D])
    w_fi = din("w_ffn_in", [depth, D, 2 * DFF])
    w_fo = din("w_ffn_out", [depth, DFF, D])
    fnw = din("final_norm_w", [D])
    out = nc.dram_tensor("out", [S, D], F32, kind="ExternalOutput").ap()
    bufA = nc.dram_tensor("bufA", [S, D], F32).ap()
    bufB = nc.dram_tensor("bufB", [S, D], F32).ap()
    dbg_out = {}
    if dbg:
        for k, shp in dbg.items():
            dbg_out[k] = nc.dram_tensor("dbg_" + k, list(shp), F32, kind="ExternalOutput").ap()

    fw = FW(nc)
    pe, act, dve, pool, sp = fw.pe, fw.act, fw.dve, fw.pool, fw.sp
    dbg_toks = []

    def dump(name, src_T, src_ap, shape):
        if name not in dbg_out:
            return
        st = fw.tile("dbgst_" + name, shape, F32)
        fw.op(dve, lambda e: e.tensor_copy(st.t[:], src_ap), [src_T], [st])
        dbg_toks.append(fw.dma(sp, dbg_out[name], st.t[:], [st], [], st, "st"))

    dram_bufs = {}

    def dchunk(name, c):
        key = (name, c)
        if key not in dram_bufs:
            dram_bufs[key] = Buf("%s_%d" % key)
        return dram_bufs[key]

    dram_ap = {"x": x_in, "A": bufA, "B": bufB, "out": out}

    PB = [fw.psum("bank%d" % i, [128, 512], F32) for i in range(8)]
    identf = fw.tile("identf", [128, 128], F32)
    ident = fw.tile("ident", [128, 128], BF16)
    tri = fw.tile("tri", [128, 128], F32)
    ones = fw.tile("ones", [128, 128], F32)
    negtri = fw.tile("negtri", [128, 4, 128], BF16)
    onesb = fw.tile("onesb", [128, 128], BF16)
    mhalf = fw.tile("mhalf", [128, 4], F32)
    cond = fw.tile("cond", [128, 8], F32)
    ctmp = fw.tile("ctmp", [128, 16], F32)

    fw.op(pool, lambda e: e.memset(identf.t[:], 1.0), [], [identf])
    fw.op(pool, lambda e: e.affine_select(out=identf.t[:], in_=identf.t[:], pattern=[[1, 128]],
                                          compare_op=ALU.is_equal, fill=0.0, base=0,
                                          channel_multiplier=-1), [identf], [identf])
    fw.op(pool, lambda e: e.tensor_copy(ident.t[:], identf.t[:]), [identf], [ident])
    fw.op(pool, lambda e: e.memset(tri.t[:], 1.0), [], [tri])
    fw.op(pool, lambda e: e.affine_select(out=tri.t[:], in_=tri.t[:], pattern=[[1, 128]],
                                          compare_op=ALU.is_ge, fill=0.0, base=0,
                                          channel_multiplier=-1), [tri], [tri])
    fw.op(pool, lambda e: e.memset(ones.t[:], 1.0), [], [ones])
    fw.op(pool, lambda e: e.memset(onesb.t[:], 1.0), [], [onesb])
    fw.op(pool, lambda e: e.memset(mhalf.t[:], -0.5), [], [mhalf])
    fw.op(pool, lambda e: e.tensor_scalar(negtri.t[:], tri.t[:].unsqueeze(1).to_broadcast([128, 4, 128]),
                                          -1.0, -NEG_BIG, ALU.add, ALU.mult), [tri], [negtri])
    fw.dma(sp, ctmp.t[:, 0:8], c_t, [], [ctmp], ctmp, "ld")
    fw.op(act, lambda e: e.activation(out=ctmp.t[:, 8:16], in_=ctmp.t[:, 0:8], func=AF.Tanh, scale=0.5),
          [ctmp], [ctmp])
    fw.op(dve, lambda e: e.scalar_tensor_tensor(out=cond.t[:], in0=ctmp.t[:, 8:16], scalar=1.0,
                                                in1=ctmp.t[:, 0:8], op0=ALU.add, op1=ALU.mult),
          [ctmp], [cond])
    fw.op(dve, lambda e: e.tensor_scalar(cond.t[:], cond.t[:], 0.5, None, ALU.mult), [cond], [cond])

    def compute_mods(l, half, mods, normw_ap):
        m0 = fw.mark()
        wa = [fw.tile("wa%d" % i, [128, 3072], F32) for i in range(2)]
        bb = fw.tile("bada_bc", [128, 3072], F32)
        c0 = half * 3072
        fw.dma(sp, bb.t[:], b_ada[l, c0:c0 + 3072].partition_broadcast(128), [], [bb], bb, "ld")
        for j in range(8):
            w = wa[j % 2]
            fw.dma(sp, w.t[:], w_ada[l, j * 128:(j + 1) * 128, c0:c0 + 3072], [], [w], w, "ld")
            for n in range(6):
                fw.op(pe, lambda e, j=j, n=n, w=w: e.matmul(
                    PB[n].t[:], lhsT=cond.t[:, j:j + 1].to_broadcast([128, 128]),
                    rhs=w.t[:, n * 512:(n + 1) * 512], start=(j == 0), stop=(j == 7)),
                    [cond, w], [PB[n]], inc=(n == 5))
        for n in range(6):
            fw.op(dve, lambda e, n=n: e.tensor_tensor(mods.t[:, n * 512:(n + 1) * 512], PB[n].t[:],
                                                       bb.t[:, n * 512:(n + 1) * 512], ALU.add),
                  [PB[n], bb], [mods])
        nwb = bb
        fw.dma(sp, nwb.t[:, 0:1024], normw_ap.partition_broadcast(128), [], [nwb], nwb, "ld")
        fw.op(dve, lambda e: e.scalar_tensor_tensor(out=mods.t[:, 1024:2048], in0=mods.t[:, 1024:2048],
                                                    scalar=1.0, in1=nwb.t[:, 0:1024],
                                                    op0=ALU.add, op1=ALU.mult), [mods, nwb], [mods])
        fw.barrier()
        fw.release(m0)

    def norm_mod_T(xt, mods, ss, t1, hb, hT, tok0, ntok_total):
        fw.op(act, lambda e: e.activation(out=hb.t[:], in_=xt.t[:], func=AF.Square,
                                          accum_out=ss.t[:, 0:1]), [xt], [hb, ss])
        fw.op(dve, lambda e: e.tensor_scalar(ss.t[:, 1:2], ss.t[:, 0:1], 1.0 / D, EPS, ALU.mult, ALU.add),
              [ss], [ss])
        fw.op(pool, lambda e: e.tensor_tensor(ss.t[:, 2:3], ss.t[:, 1:2], mhalf.t[:, 0:1], ALU.pow),
              [ss, mhalf], [ss])
        fw.op(dve, lambda e: e.scalar_tensor_tensor(out=t1.t[:], in0=xt.t[:], scalar=ss.t[:, 2:3],
                                                    in1=mods.t[:, 1024:2048], op0=ALU.mult, op1=ALU.mult),
              [xt, ss, mods], [t1])
        fw.op(pool, lambda e: e.tensor_tensor(hb.t[:], t1.t[:], mods.t[:, 0:1024], ALU.add),
              [t1, mods], [hb])
        pT = PB[0].t[:].bitcast(BF16)
        for j in range(8):
            fw.op(pe, lambda e, j=j: e.transpose(pT[:, j * 128:(j + 1) * 128], hb.t[:, j * 128:(j + 1) * 128],
                                                 ident.t[:]), [hb, ident], [PB[0]], inc=(j == 7))
        fw.op(act, lambda e: e.activation(out=hT.t[:, :, tok0:tok0 + 128],
                                          in_=pT.rearrange("p (j t) -> p j t", j=8), func=AF.Copy),
              [PB[0]], [hT])

    def outproj_residual(ysrc, yT, Wo, mods, xres, t2, dst_name, c, bankT, banks):
        pT = PB[bankT].t[:].bitcast(BF16)
        for j in range(8):
            fw.op(pe, lambda e, j=j: e.transpose(pT[:, j * 128:(j + 1) * 128], ysrc.t[:, j * 128:(j + 1) * 128],
                                                 ident.t[:]), [ysrc, ident], [PB[bankT]], inc=(j == 7))
        fw.op(act, lambda e: e.activation(out=yT.t[:], in_=pT.rearrange("p (j t) -> p j t", j=8), func=AF.Copy),
              [PB[bankT]], [yT])
        for nb in range(2):
            bk = PB[banks[nb]]
            for j in range(8):
                fw.op(pe, lambda e, j=j, nb=nb, bk=bk: e.matmul(bk.t[:], lhsT=yT.t[:, j, :],
                                                             rhs=Wo.t[:, j, nb * 512:(nb + 1) * 512],
                                                             start=(j == 0), stop=(j == 7)),
                      [yT, Wo], [bk], inc=(j == 7))
            fw.op(dve, lambda e, nb=nb, bk=bk: e.tensor_tensor(t2.t[:, nb * 512:(nb + 1) * 512], bk.t[:],
                                                                mods.t[:, 2048 + nb * 512:2048 + (nb + 1) * 512],
                                                                ALU.mult), [bk, mods], [t2])
        fw.op(pool, lambda e: e.tensor_tensor(xres.t[:], xres.t[:], t2.t[:], ALU.add), [xres, t2], [xres])
        db = dchunk(dst_name, c)
        fw.dma(sp, dram_ap[dst_name][c * 128:(c + 1) * 128, :], xres.t[:], [xres], [db], xres, "st")

    def build_diag(diag, cw, nf):
        for f in range(nf):
            for j in range(5):
                eng = pool if (f * 5 + j) % 2 == 0 else dve
                fw.op(eng, lambda e, f=f, j=j: e.tensor_scalar(diag.t[:, f, j, :], identf.t[:],
                                                               cw.t[:, f, j:j + 1], 0.5, ALU.mult, ALU.mult),
                      [identf, cw], [diag])

    def conv_silu(U, diag, dst, f0, nf4, banks):
        for g4 in range(nf4):
            bk = PB[banks[g4]]
            for ff in range(4):
                f = f0 + g4 * 4 + ff
                for j in range(5):
                    rhs = U.t[:, f, j:j + 128] if j < 4 else onesb.t[:]
                    fw.op(pe, lambda e, f=f, j=j, ff=ff, bk=bk, rhs=rhs: e.matmul(
                        bk.t[:, ff * 128:(ff + 1) * 128], lhsT=diag.t[:, f, j, :], rhs=rhs,
                        start=(j == 0), stop=(j == 4)), [diag, U, onesb], [bk], inc=(ff == 3 and j == 4))
        return

    src_name = "x"
    for l in range(depth):
        dst_name = "A" if (l % 2 == 0) else "B"
        last = (l == depth - 1)
        mL = fw.mark()
        mods = fw.tile("mods", [128, 3072], F32)
        compute_mods(l, 0, mods, norm1_w[l])
        smallbc = fw.tile("smallbc", [128, 64], F32)
        fw.dma(sp, smallbc.t[:], small[l].partition_broadcast(128), [], [smallbc], smallbc, "ld")

        mM = fw.mark()
        Wm = fw.tile("Wm", [128, 8, 3080], BF16)
        Wo = fw.tile("Wo", [128, 8, 1024], BF16)
        for j in range(8):
            fw.dma(pool, Wm.t[:, j, :], w_in[l, j * 128:(j + 1) * 128, 0:3080], [], [Wm], Wm, "ld")
        for j in range(8):
            fw.dma(pool, Wo.t[:, j, :], w_out[l, j * 128:(j + 1) * 128, :], [], [Wo], Wo, "ld")
        cw = fw.tile("cw", [128, 8, 5], F32)
        fw.dma(sp, cw.t[:], cqk[l], [], [cw], cw, "ld")
        diag = fw.tile("diag", [128, 8, 5, 128], BF16)
        build_diag(diag, cw, 8)
        nwh = fw.tile("nwh", [128, 1024], F32)
        fw.dma(sp, nwh.t[:], mlnw[l].partition_broadcast(128), [], [nwh], nwh, "ld")
        fw.op(pool, lambda e: e.tensor_scalar(nwh.t[:], nwh.t[:], 0.5, None, ALU.mult), [nwh], [nwh])
        xts = [fw.tile("xt%d" % i, [128, 1024], F32) for i in range(2)]
        ss = fw.tile("ss", [128, 4], F32)
        t1 = fw.tile("t1", [128, 1024], F32)
        t2 = fw.tile("t2", [128, 1024], F32)
        hb = fw.tile("hb", [128, 1024], BF16)
        hT = fw.tile("hT", [128, 8, 128], BF16)
        U = fw.tile("U", [128, 8, 132], BF16)
        qkf = fw.tile("qkf", [128, 8, 128], BF16)
        thq = fw.tile("thq", [128, 512], F32)
        kTM = fw.tile("kTM", [128, 4, 128], BF16)
        vp = fw.tile("vp", [128, 4, 258], BF16)
        og = fw.tile("og", [128, 1024], F32)
        Sm = fw.tile("Sm", [128, 4, 128], BF16)
        Cst = fw.tile("Cst", [128, 4, 256], F32)
        nst = fw.tile("nst", [128, 4], F32)
        Cb = fw.tile("Cb", [128, 4, 258], BF16)
        gts = fw.tile("gts", [128, 48], F32)
        ymlb = fw.tile("ymlb", [128, 1024], BF16)
        yT = fw.tile("yT", [128, 8, 128], BF16)
        jk = fw.tile("jk", [128, 256], BF16)
        fw.op(pool, lambda e: e.memset(U.t[:], 0.0), [], [U])
        fw.op(pool, lambda e: e.memset(Cst.t[:], 0.0), [], [Cst])
        fw.op(pool, lambda e: e.memset(nst.t[:], 0.0), [], [nst])
        fw.op(pool, lambda e: e.memset(Cb.t[:], 0.0), [], [Cb])
        LNS = math.log(128.0 ** -0.5)
        lns = fw.tile("lns", [128, 1], F32)
        fw.op(pool, lambda e: e.memset(lns.t[:], LNS), [], [lns])

        for c in range(nch):
            xt = xts[c % 2]
            sb = dchunk(src_name, c)
            fw.dma(sp, xt.t[:], dram_ap[src_name][c * 128:(c + 1) * 128, :], [sb], [xt], xt, "ld")
            norm_mod_T(xt, mods, ss, t1, hb, hT, 0, 128)
            for j in range(8):
                fw.op(pe, lambda e, j=j: e.matmul(PB[1].t[:, 0:8], lhsT=hT.t[:, j, :], rhs=Wm.t[:, j, 3072:3080],
                                                  start=(j == 0), stop=(j == 7)), [hT, Wm], [PB[1]], inc=(j == 7))
            for f in range(8):
                bk = PB[2 + f // 4]
                for j in range(8):
                    fw.op(pe, lambda e, j=j, f=f, bk=bk: e.matmul(bk.t[:, (f % 4) * 128:(f % 4 + 1) * 128],
                                                                 lhsT=Wm.t[:, j, f * 128:(f + 1) * 128],
                                                                 rhs=hT.t[:, j, :], start=(j == 0), stop=(j == 7)),
                          [hT, Wm], [bk], inc=(j == 7 and f % 4 == 3))
            for nb in range(4):
                bk = PB[4 + nb]
                for j in range(8):
                    fw.op(pe, lambda e, j=j, nb=nb, bk=bk: e.matmul(bk.t[:], lhsT=hT.t[:, j, :],
                                                                   rhs=Wm.t[:, j, 1024 + nb * 512:1024 + (nb + 1) * 512],
                                                                   start=(j == 0), stop=(j == 7)),
                          [hT, Wm], [bk], inc=(j == 7))
            fw.op(dve, lambda e: e.tensor_tensor(gts.t[:, 0:8], PB[1].t[:, 0:8], smallbc.t[:, 0:8], ALU.add),
                  [PB[1], smallbc], [gts])
            fw.op(act, lambda e: e.activation(out=gts.t[:, 8:12], in_=gts.t[:, 4:8], func=AF.Exp, scale=-1.0),
                  [gts], [gts])
            fw.op(act, lambda e: e.activation(out=gts.t[:, 12:16], in_=gts.t[:, 8:12], func=AF.Ln, bias=ones.t[:, 0:1]),
                  [gts, ones], [gts])
            fw.op(pe, lambda e: e.matmul(PB[1].t[:, 8:12], lhsT=tri.t[:], rhs=gts.t[:, 12:16], start=True, stop=True),
                  [tri, gts], [PB[1]], inc=False)
            fw.op(pe, lambda e: e.matmul(PB[1].t[:, 12:16], lhsT=ones.t[:], rhs=gts.t[:, 12:16], start=True, stop=True),
                  [ones, gts], [PB[1]])
            fw.op(dve, lambda e: e.tensor_tensor(gts.t[:, 16:20], gts.t[:, 0:4], PB[1].t[:, 8:12], ALU.add),
                  [gts, PB[1]], [gts])
            fw.op(act, lambda e: e.activation(out=gts.t[:, 20:24], in_=gts.t[:, 16:20], func=AF.Exp), [gts], [gts])
            fw.op(act, lambda e: e.activation(out=gts.t[:, 24:28], in_=PB[1].t[:, 8:12], func=AF.Exp, scale=-1.0,
                                              bias=lns.t[:, 0:1]), [PB[1], lns], [gts])
            fw.op(act, lambda e: e.activation(out=gts.t[:, 28:32], in_=PB[1].t[:, 12:16], func=AF.Exp, scale=-1.0),
                  [PB[1]], [gts])
            for nb in range(2):
                fw.op(dve, lambda e, nb=nb: e.tensor_tensor(
                    vp.t[:, 2 * nb:2 * nb + 2, 0:256], PB[4 + nb].t[:].rearrange("p (h v) -> p h v", h=2),
                    gts.t[:, 20 + 2 * nb:22 + 2 * nb].unsqueeze(2).to_broadcast([128, 2, 256]), ALU.mult),
                    [PB[4 + nb], gts], [vp])
            fw.op(dve, lambda e: e.tensor_copy(vp.t[:, :, 256:257], gts.t[:, 20:24].unsqueeze(2)), [gts], [vp])
            for nb in range(2):
                fw.op(act, lambda e, nb=nb: e.activation(out=og.t[:, nb * 512:(nb + 1) * 512], in_=PB[6 + nb].t[:],
                                                         func=AF.Tanh, scale=0.5), [PB[6 + nb]], [og])
            fw.op(dve, lambda e: e.scalar_tensor_tensor(out=og.t[:], in0=og.t[:], scalar=1.0, in1=nwh.t[:],
                                                         op0=ALU.add, op1=ALU.mult), [og, nwh], [og])
            for g4 in range(2):
                fw.op(act, lambda e, g4=g4: e.activation(out=U.t[:, g4 * 4:(g4 + 1) * 4, 3:131],
                                                         in_=PB[2 + g4].t[:].rearrange("p (f t) -> p f t", f=4),
                                                         func=AF.Copy), [PB[2 + g4]], [U])
            conv_silu(U, diag, qkf, 0, 2, [2, 3])
            for g4 in range(2):
                fw.op(act, lambda e, g4=g4: e.activation(out=thq.t[:], in_=PB[2 + g4].t[:], func=AF.Tanh),
                      [PB[2 + g4]], [thq])
                fw.op(dve, lambda e, g4=g4: e.scalar_tensor_tensor(
                    out=qkf.t[:, g4 * 4:(g4 + 1) * 4, :].rearrange("p f t -> p (f t)"), in0=thq.t[:], scalar=1.0,
                    in1=PB[2 + g4].t[:], op0=ALU.add, op1=ALU.mult), [thq, PB[2 + g4]], [qkf])
            fw.op(pool, lambda e: e.tensor_copy(U.t[:, :, 0:3], U.t[:, :, 128:131]), [U], [U])
            pT0 = PB[0].t[:].bitcast(BF16)
            for h in range(4):
                fw.op(pe, lambda e, h=h: e.transpose(pT0[:, h * 128:(h + 1) * 128], qkf.t[:, 4 + h, :], ident.t[:]),
                      [qkf, ident], [PB[0]], inc=(h == 3))
            fw.op(act, lambda e: e.activation(out=kTM.t[:], in_=pT0[:, 0:512].rearrange("p (h d) -> p h d", h=4),
                                              func=AF.Copy), [PB[0]], [kTM])
            for h in range(4):
                fw.op(pe, lambda e, h=h: e.matmul(PB[2].t[:, h * 128:(h + 1) * 128], lhsT=qkf.t[:, 4 + h, :],
                                                  rhs=qkf.t[:, h, :], start=True, stop=True), [qkf], [PB[2]],
                      inc=(h == 3))
            fw.op(dve, lambda e: e.tensor_tensor(Sm.t[:], PB[2].t[:].rearrange("p (h t) -> p h t", h=4),
                                                 tri.t[:].unsqueeze(1).to_broadcast([128, 4, 128]), ALU.mult),
                  [PB[2], tri], [Sm])
            for h in range(4):
                bk = PB[4 + h // 2]
                o0 = (h % 2) * 256
                fw.op(pe, lambda e, h=h, bk=bk, o0=o0: e.matmul(bk.t[:, o0:o0 + 256], lhsT=Sm.t[:, h, :],
                                                               rhs=vp.t[:, h, 0:256], start=True, stop=False),
                      [Sm, vp], [bk], inc=False)
                fw.op(pe, lambda e, h=h, bk=bk, o0=o0: e.matmul(bk.t[:, o0:o0 + 256], lhsT=qkf.t[:, h, :],
                                                               rhs=Cb.t[:, h, 0:256], start=False, stop=True),
                      [qkf, Cb], [bk], inc=False)
                fw.op(pe, lambda e, h=h: e.matmul(PB[3].t[:, h:h + 1], lhsT=Sm.t[:, h, :], rhs=vp.t[:, h, 256:257],
                                                  start=True, stop=False), [Sm, vp], [PB[3]], inc=False)
                fw.op(pe, lambda e, h=h: e.matmul(PB[3].t[:, h:h + 1], lhsT=qkf.t[:, h, :], rhs=Cb.t[:, h, 256:257],
                                                  start=False, stop=True), [qkf, Cb], [PB[3]], inc=(h == 3))
            for h in range(4):
                bk = PB[6 + h // 2]
                o0 = (h % 2) * 256
                fw.op(pe, lambda e, h=h, bk=bk, o0=o0: e.matmul(bk.t[:, o0:o0 + 256], lhsT=kTM.t[:, h, :],
                                                               rhs=vp.t[:, h, 0:256], start=True, stop=True),
                      [kTM, vp], [bk], inc=False)
                fw.op(pe, lambda e, h=h: e.matmul(PB[3].t[:, 4 + h:5 + h], lhsT=kTM.t[:, h, :], rhs=vp.t[:, h, 256:257],
                                                  start=True, stop=True), [kTM, vp], [PB[3]], inc=(h == 3))
            fw.op(dve, lambda e: e.tensor_tensor(gts.t[:, 32:36], PB[3].t[:, 0:4], gts.t[:, 24:28], ALU.mult),
                  [PB[3], gts], [gts])
            fw.op(dve, lambda e: e.scalar_tensor_tensor(out=gts.t[:, 32:36], in0=gts.t[:, 32:36], scalar=-1.0,
                                                        in1=gts.t[:, 32:36], op0=ALU.mult, op1=ALU.max),
                  [gts], [gts])
            fw.op(dve, lambda e: e.tensor_scalar(gts.t[:, 32:36], gts.t[:, 32:36], 1.0, None, ALU.max),
                  [gts], [gts])
            fw.op(dve, lambda e: e.reciprocal(gts.t[:, 32:36], gts.t[:, 32:36]), [gts], [gts])
            fw.op(dve, lambda e: e.tensor_tensor(gts.t[:, 36:40], gts.t[:, 32:36], gts.t[:, 24:28], ALU.mult),
                  [gts], [gts])
            for h in range(4):
                bk = PB[4 + h // 2]
                o0 = (h % 2) * 256
                fw.op(act, lambda e, h=h, bk=bk, o0=o0: e.activation(out=jk.t[:], in_=bk.t[:, o0:o0 + 256],
                                                                    func=AF.Square, accum_out=gts.t[:, 40 + h:41 + h]),
                      [bk], [jk, gts])
            fw.op(dve, lambda e: e.tensor_tensor(gts.t[:, 44:48], gts.t[:, 36:40], gts.t[:, 36:40], ALU.mult),
                  [gts], [gts])
            fw.op(dve, lambda e: e.tensor_tensor(gts.t[:, 44:48], gts.t[:, 44:48], gts.t[:, 40:44], ALU.mult),
                  [gts], [gts])
            fw.op(dve, lambda e: e.tensor_scalar(gts.t[:, 44:48], gts.t[:, 44:48], 1.0 / 256, EPS, ALU.mult, ALU.add),
                  [gts], [gts])
            fw.op(pool, lambda e: e.tensor_tensor(gts.t[:, 44:48], gts.t[:, 44:48], mhalf.t[:, 0:4], ALU.pow),
                  [gts, mhalf], [gts])
            fw.op(dve, lambda e: e.tensor_tensor(gts.t[:, 44:48], gts.t[:, 44:48], gts.t[:, 36:40], ALU.mult),
                  [gts], [gts])
            for h in range(4):
                bk = PB[4 + h // 2]
                o0 = (h % 2) * 256
                fw.op(dve, lambda e, h=h, bk=bk, o0=o0: e.scalar_tensor_tensor(
                    out=ymlb.t[:, h * 256:(h + 1) * 256], in0=bk.t[:, o0:o0 + 256], scalar=gts.t[:, 44 + h:45 + h],
                    in1=og.t[:, h * 256:(h + 1) * 256], op0=ALU.mult, op1=ALU.mult), [bk, gts, og], [ymlb])
            for nb in range(2):
                fw.op(dve, lambda e, nb=nb: e.tensor_tensor(
                    Cst.t[:, 2 * nb:2 * nb + 2, :], Cst.t[:, 2 * nb:2 * nb + 2, :],
                    PB[6 + nb].t[:].rearrange("p (h v) -> p h v", h=2), ALU.add), [Cst, PB[6 + nb]], [Cst])
            fw.op(pool, lambda e: e.tensor_tensor(Cst.t[:], Cst.t[:],
                                                  gts.t[:, 28:32].unsqueeze(2).to_broadcast([128, 4, 256]), ALU.mult),
                  [Cst, gts], [Cst])
            fw.op(dve, lambda e: e.tensor_tensor(nst.t[:], nst.t[:], PB[3].t[:, 4:8], ALU.add), [nst, PB[3]], [nst])
            fw.op(dve, lambda e: e.tensor_tensor(nst.t[:], nst.t[:], gts.t[:, 28:32], ALU.mult), [nst, gts], [nst])
            fw.op(act, lambda e: e.activation(out=Cb.t[:, :, 0:256], in_=Cst.t[:], func=AF.Copy), [Cst], [Cb])
            fw.op(dve, lambda e: e.tensor_copy(Cb.t[:, :, 256:257], nst.t[:].unsqueeze(2)), [nst], [Cb])
            if c == 0 and l == 0:
                dump("ymlb", ymlb, ymlb.t[:], [128, 1024])
                dump("gts", gts, gts.t[:], [128, 48])
            outproj_residual(ymlb, yT, Wo, mods, xt, t2, dst_name, c, 0, [4, 5])
        fw.barrier()
        fw.release(mM)

        Ws = fw.tile("Ws", [128, 8, 2576], BF16)
        Wo = fw.tile("Wo2", [128, 8, 1024], BF16)
        for j in range(8):
            fw.dma(pool, Ws.t[:, j, :], w_in[l, j * 128:(j + 1) * 128, 3080:5656], [], [Ws], Ws, "ld")
        for j in range(8):
            fw.dma(pool, Wo.t[:, j, :], w_out[l, 1024 + j * 128:1024 + (j + 1) * 128, :], [], [Wo], Wo, "ld")
        cw = fw.tile("cw2", [128, 12, 5], F32)
        fw.dma(sp, cw.t[:], cssm[l], [], [cw], cw, "ld")
        diag = fw.tile("diag2", [128, 12, 5, 128], BF16)
        build_diag(diag, cw, 12)
        nwh = fw.tile("nwh2", [128, 1024], F32)
        fw.dma(sp, nwh.t[:], ssnw[l].partition_broadcast(128), [], [nwh], nwh, "ld")
        fw.op(pool, lambda e: e.tensor_scalar(nwh.t[:], nwh.t[:], 0.5, None, ALU.mult), [nwh], [nwh])
        xts = [fw.tile("sxt%d" % i, [128, 1024], F32) for i in range(2)]
        xrs = [fw.tile("sxr%d" % i, [128, 1024], F32) for i in range(2)]
        ss = fw.tile("sss", [128, 8], F32)
        t1 = fw.tile("st1", [128, 1024], F32)
        t2 = fw.tile("st2", [128, 1024], F32)
        hb = fw.tile("shb", [128, 1024], BF16)
        hT = fw.tile("shT", [128, 8, 128], BF16)
        U = fw.tile("sU", [128, 12, 132], BF16)
        xbf = fw.tile("xbf", [128, 12, 128], BF16)
        thq = fw.tile("sthq", [128, 512], F32)
        Btm = fw.tile("Btm", [128, 2, 128], BF16)
        cbb = fw.tile("cbb", [128, 2, 128], BF16)
        xdt = fw.tile("xdt", [128, 1024], BF16)
        xD = fw.tile("xD", [128, 1024], BF16)
        xw = fw.tile("xw", [128, 1024], BF16)
        gd = fw.tile("gd", [128, 128], F32)
        rx = fw.tile("rx", [128, 8, 128], F32)
        df = fw.tile("df", [128, 8, 128], F32)
        Eb = fw.tile("Eb", [128, 8, 128], BF16)
        mix = fw.tile("mix", [128, 8, 128], BF16)
        thz = fw.tile("thz", [128, 1024], F32)
        yss = fw.tile("yss", [128, 1024], F32)
        Sst = fw.tile("Sst", [128, 1024], F32)
        Sb = fw.tile("Sb", [128, 1024], BF16)
        yssb = fw.tile("yssb", [128, 1024], BF16)
        yT = fw.tile("syT", [128, 8, 128], BF16)
        fw.op(pool, lambda e: e.memset(U.t[:], 0.0), [], [U])
        fw.op(pool, lambda e: e.memset(Sst.t[:], 0.0), [], [Sst])
        fw.op(pool, lambda e: e.memset(Sb.t[:], 0.0), [], [Sb])
        aneg = fw.tile("aneg", [128, 16], F32)
        fw.op(act, lambda e: e.activation(out=aneg.t[:], in_=smallbc.t[:, 24:40], func=AF.Exp), [smallbc], [aneg])
        fw.op(dve, lambda e: e.tensor_scalar(aneg.t[:], aneg.t[:], -1.0, None, ALU.mult), [aneg], [aneg])

        for c in range(nch):
            xt = xts[c % 2]
            xr = xrs[c % 2]
            sb = dchunk(src_name, c)
            fw.dma(sp, xt.t[:], dram_ap[src_name][c * 128:(c + 1) * 128, :], [sb], [xt], xt, "ld")
            db = dchunk(dst_name, c)
            fw.dma(sp, xr.t[:], dram_ap[dst_name][c * 128:(c + 1) * 128, :], [db], [xr], xr, "ld")
            norm_mod_T(xt, mods, ss, t1, hb, hT, 0, 128)
            for j in range(8):
                fw.op(pe, lambda e, j=j: e.matmul(PB[1].t[:, 0:16], lhsT=hT.t[:, j, :], rhs=Ws.t[:, j, 2560:2576],
                                                  start=(j == 0), stop=(j == 7)), [hT, Ws], [PB[1]], inc=(j == 7))
            for f in range(12):
                bk = PB[2 + f // 4]
                for j in range(8):
                    fw.op(pe, lambda e, j=j, f=f, bk=bk: e.matmul(bk.t[:, (f % 4) * 128:(f % 4 + 1) * 128],
                                                                 lhsT=Ws.t[:, j, 1024 + f * 128:1024 + (f + 1) * 128],
                                                                 rhs=hT.t[:, j, :], start=(j == 0), stop=(j == 7)),
                          [hT, Ws], [bk], inc=(j == 7 and f % 4 == 3))
            for nb in range(2):
                bk = PB[5 + nb]
                for j in range(8):
                    fw.op(pe, lambda e, j=j, nb=nb, bk=bk: e.matmul(bk.t[:], lhsT=hT.t[:, j, :],
                                                                   rhs=Ws.t[:, j, nb * 512:(nb + 1) * 512],
                                                                   start=(j == 0), stop=(j == 7)),
                          [hT, Ws], [bk], inc=(j == 7))
            fw.op(dve, lambda e: e.tensor_tensor(gd.t[:, 0:16], PB[1].t[:, 0:16], smallbc.t[:, 8:24], ALU.add),
                  [PB[1], smallbc], [gd])
            fw.op(act, lambda e: e.activation(out=gd.t[:, 16:32], in_=gd.t[:, 0:16], func=AF.Exp), [gd], [gd])
            fw.op(act, lambda e: e.activation(out=gd.t[:, 32:48], in_=gd.t[:, 16:32], func=AF.Ln, bias=ones.t[:, 0:1]),
                  [gd], [gd])
            fw.op(dve, lambda e: e.tensor_tensor(gd.t[:, 48:64], gd.t[:, 32:48], aneg.t[:], ALU.mult),
                  [gd, aneg], [gd])
            fw.op(pe, lambda e: e.matmul(PB[1].t[:, 16:32], lhsT=tri.t[:], rhs=gd.t[:, 48:64], start=True, stop=True),
                  [tri, gd], [PB[1]], inc=False)
            fw.op(pe, lambda e: e.matmul(PB[1].t[:, 32:48], lhsT=ones.t[:], rhs=gd.t[:, 48:64], start=True, stop=True),
                  [ones, gd], [PB[1]])
            fw.op(dve, lambda e: e.tensor_copy(gd.t[:, 64:96], PB[1].t[:, 16:48]), [PB[1]], [gd])
            fw.op(act, lambda e: e.activation(out=gd.t[:, 96:112], in_=gd.t[:, 64:80], func=AF.Exp), [gd], [gd])
            fw.op(dve, lambda e: e.tensor_tensor(gd.t[:, 112:128], gd.t[:, 80:96], gd.t[:, 64:80], ALU.subtract),
                  [gd], [gd])
            fw.op(act, lambda e: e.activation(out=gd.t[:, 112:128], in_=gd.t[:, 112:128], func=AF.Exp), [gd], [gd])
            for nb in range(2):
                fw.op(act, lambda e, nb=nb: e.activation(out=thz.t[:, nb * 512:(nb + 1) * 512], in_=PB[5 + nb].t[:],
                                                         func=AF.Tanh, scale=0.5), [PB[5 + nb]], [thz])
                fw.op(dve, lambda e, nb=nb: e.scalar_tensor_tensor(out=thz.t[:, nb * 512:(nb + 1) * 512],
                                                                    in0=thz.t[:, nb * 512:(nb + 1) * 512], scalar=1.0,
                                                                    in1=PB[5 + nb].t[:], op0=ALU.add, op1=ALU.mult),
                      [thz, PB[5 + nb]], [thz])
            for g4 in range(3):
                fw.op(act, lambda e, g4=g4: e.activation(out=U.t[:, g4 * 4:(g4 + 1) * 4, 3:131],
                                                         in_=PB[2 + g4].t[:].rearrange("p (f t) -> p f t", f=4),
                                                         func=AF.Copy), [PB[2 + g4]], [U])
            conv_silu(U, diag, xbf, 0, 3, [2, 3, 4])
            for g4 in range(3):
                fw.op(act, lambda e, g4=g4: e.activation(out=thq.t[:], in_=PB[2 + g4].t[:], func=AF.Tanh),
                      [PB[2 + g4]], [thq])
                fw.op(dve, lambda e, g4=g4: e.scalar_tensor_tensor(
                    out=xbf.t[:, g4 * 4:(g4 + 1) * 4, :].rearrange("p f t -> p (f t)"), in0=thq.t[:], scalar=1.0,
                    in1=PB[2 + g4].t[:], op0=ALU.add, op1=ALU.mult), [thq, PB[2 + g4]], [xbf])
            fw.op(pool, lambda e: e.tensor_copy(U.t[:, :, 0:3], U.t[:, :, 128:131]), [U], [U])
            pT0 = PB[0].t[:].bitcast(BF16)
            for f in range(8):
                fw.op(pe, lambda e, f=f: e.transpose(pT0[:, f * 128:(f + 1) * 128], xbf.t[:, f, :], ident.t[:]),
                      [xbf, ident], [PB[0]], inc=(f == 7))
            pT7 = PB[7].t[:].bitcast(BF16)
            for g in range(2):
                fw.op(pe, lambda e, g=g: e.transpose(pT7[:, g * 128:(g + 1) * 128], xbf.t[:, 8 + g, :], ident.t[:]),
                      [xbf, ident], [PB[7]], inc=(g == 1))
            xs3 = pT0.rearrange("p (h q) -> p h q", h=16)
            fw.op(dve, lambda e: e.tensor_tensor(xdt.t[:].rearrange("p (h q) -> p h q", h=16), xs3,
                                                 gd.t[:, 32:48].unsqueeze(2).to_broadcast([128, 16, 64]), ALU.mult),
                  [PB[0], gd], [xdt])
            fw.op(dve, lambda e: e.tensor_tensor(xD.t[:].rearrange("p (h q) -> p h q", h=16), xs3,
                                                 smallbc.t[:, 40:56].unsqueeze(2).to_broadcast([128, 16, 64]), ALU.mult),
                  [PB[0], smallbc], [xD])
            fw.op(dve, lambda e: e.tensor_tensor(xw.t[:].rearrange("p (h q) -> p h q", h=16),
                                                 xdt.t[:].rearrange("p (h q) -> p h q", h=16),
                                                 gd.t[:, 112:128].unsqueeze(2).to_broadcast([128, 16, 64]), ALU.mult),
                  [xdt, gd], [xw])
            fw.op(act, lambda e: e.activation(out=Btm.t[:], in_=pT7[:, 0:256].rearrange("p (g n) -> p g n", g=2),
                                              func=AF.Copy), [PB[7]], [Btm])
            for g in range(2):
                fw.op(pe, lambda e, g=g: e.matmul(PB[7].t[:, 256 + g * 128:256 + (g + 1) * 128], lhsT=xbf.t[:, 8 + g, :],
                                                  rhs=xbf.t[:, 10 + g, :], start=True, stop=True), [xbf], [PB[7]],
                      inc=(g == 1))
            fw.op(act, lambda e: e.activation(out=cbb.t[:], in_=PB[7].t[:, 256:512].rearrange("p (g t) -> p g t", g=2),
                                              func=AF.Copy), [PB[7]], [cbb])
            for g in range(2):
                fw.op(dve, lambda e, g=g: e.tensor_tensor(rx.t[:], tri.t[:].unsqueeze(1).to_broadcast([128, 8, 128]),
                                                          gd.t[:, 48 + 8 * g:56 + 8 * g].unsqueeze(2).to_broadcast([128, 8, 128]),
                                                          ALU.mult), [tri, gd], [rx])
                for q in range(2):
                    bk = PB[5 + q]
                    fw.op(pe, lambda e, q=q, bk=bk: e.matmul(bk.t[:], lhsT=ones.t[:],
                                                            rhs=rx.t[:, 4 * q:4 * q + 4, :].rearrange("p h t -> p (h t)"),
                                                            start=True, stop=False), [ones, rx], [bk], inc=False)
                    fw.op(pe, lambda e, q=q, bk=bk: e.matmul(bk.t[:], lhsT=ident.t[:],
                                                            rhs=negtri.t[:].rearrange("p h t -> p (h t)"),
                                                            start=False, stop=True), [ident, negtri], [bk], inc=(q == 1))
                for q in range(2):
                    fw.op(dve, lambda e, q=q, g=g: e.tensor_tensor(
                        df.t[:, 4 * q:4 * q + 4, :], PB[5 + q].t[:].rearrange("p (h t) -> p h t", h=4),
                        gd.t[:, 64 + 8 * g + 4 * q:64 + 8 * g + 4 * q + 4].unsqueeze(2).to_broadcast([128, 4, 128]),
                        ALU.subtract), [PB[5 + q], gd], [df])
                fw.op(act, lambda e: e.activation(out=Eb.t[:], in_=df.t[:], func=AF.Exp), [df], [Eb])
                fw.op(dve, lambda e, g=g: e.tensor_tensor(mix.t[:], Eb.t[:],
                                                          cbb.t[:, g, :].unsqueeze(1).to_broadcast([128, 8, 128]),
                                                          ALU.mult), [Eb, cbb], [mix])
                for hh in range(8):
                    c0 = (g * 8 + hh) * 64
                    fw.op(pe, lambda e, hh=hh, c0=c0: e.matmul(PB[2].t[:, hh * 64:(hh + 1) * 64], lhsT=ident.t[:],
                                                              rhs=xD.t[:, c0:c0 + 64], start=True, stop=False),
                          [ident, xD], [PB[2]], inc=False)
                    fw.op(pe, lambda e, hh=hh, c0=c0: e.matmul(PB[2].t[:, hh * 64:(hh + 1) * 64], lhsT=mix.t[:, hh, :],
                                                              rhs=xdt.t[:, c0:c0 + 64], start=False, stop=True),
                          [mix, xdt], [PB[2]], inc=(hh == 7))
                fw.op(pe, lambda e, g=g: e.matmul(PB[3].t[:], lhsT=xbf.t[:, 10 + g, :], rhs=Sb.t[:, g * 512:(g + 1) * 512],
                                                  start=True, stop=True), [xbf, Sb], [PB[3]], inc=False)
                fw.op(pe, lambda e, g=g: e.matmul(PB[4].t[:], lhsT=Btm.t[:, g, :], rhs=xw.t[:, g * 512:(g + 1) * 512],
                                                  start=True, stop=True), [Btm, xw], [PB[4]])
                fw.op(dve, lambda e, g=g: e.tensor_tensor(
                    t1.t[:, 0:512].rearrange("p (h q) -> p h q", h=8), PB[3].t[:].rearrange("p (h q) -> p h q", h=8),
                    gd.t[:, 96 + 8 * g:104 + 8 * g].unsqueeze(2).to_broadcast([128, 8, 64]), ALU.mult),
                    [PB[3], gd], [t1])
                fw.op(dve, lambda e, g=g: e.tensor_tensor(yss.t[:, g * 512:(g + 1) * 512], PB[2].t[:], t1.t[:, 0:512],
                                                          ALU.add), [PB[2], t1], [yss])
                fw.op(act, lambda e, g=g: e.activation(out=ss.t[:, 0:8], in_=gd.t[:, 80 + 8 * g:88 + 8 * g], func=AF.Exp),
                      [gd], [ss])
                fw.op(pool, lambda e, g=g: e.tensor_tensor(
                    Sst.t[:, g * 512:(g + 1) * 512].rearrange("p (h q) -> p h q", h=8),
                    Sst.t[:, g * 512:(g + 1) * 512].rearrange("p (h q) -> p h q", h=8),
                    ss.t[:, 0:8].unsqueeze(2).to_broadcast([128, 8, 64]), ALU.mult), [Sst, ss], [Sst])
                fw.op(dve, lambda e, g=g: e.tensor_tensor(Sst.t[:, g * 512:(g + 1) * 512], Sst.t[:, g * 512:(g + 1) * 512],
                                                          PB[4].t[:], ALU.add), [Sst, PB[4]], [Sst])
                fw.op(act, lambda e, g=g: e.activation(out=Sb.t[:, g * 512:(g + 1) * 512], in_=Sst.t[:, g * 512:(g + 1) * 512],
                                                       func=AF.Copy), [Sst], [Sb])
            fw.op(pool, lambda e: e.tensor_tensor(yss.t[:], yss.t[:], thz.t[:], ALU.mult), [yss, thz], [yss])
            fw.op(act, lambda e: e.activation(out=hb.t[:], in_=yss.t[:], func=AF.Square, accum_out=ss.t[:, 4:5]),
                  [yss], [hb, ss])
            fw.op(dve, lambda e: e.tensor_scalar(ss.t[:, 5:6], ss.t[:, 4:5], 0.25 / D, EPS, ALU.mult, ALU.add),
                  [ss], [ss])
            fw.op(pool, lambda e: e.tensor_tensor(ss.t[:, 6:7], ss.t[:, 5:6], mhalf.t[:, 0:1], ALU.pow),
                  [ss, mhalf], [ss])
            fw.op(dve, lambda e: e.scalar_tensor_tensor(out=yssb.t[:], in0=yss.t[:], scalar=ss.t[:, 6:7], in1=nwh.t[:],
                                                        op0=ALU.mult, op1=ALU.mult), [yss, ss, nwh], [yssb])
            if c == 0 and l == 0:
                dump("yssb", yssb, yssb.t[:], [128, 1024])
                dump("gd", gd, gd.t[:], [128, 128])
            outproj_residual(yssb, yT, Wo, mods, xr, t2, dst_name, c, 0, [5, 6])
        fw.barrier()
        fw.release(mL)

        mF = fw.mark()
        mods = fw.tile("modsF", [128, 3072], F32)
        compute_mods(l, 1, mods, norm2_w[l])
        Wfi = fw.tile("Wfi", [128, 8, 2 * DFF], BF16)
        Wfo = fw.tile("Wfo", [128, 22, 1024], BF16)
        for j in range(8):
            fw.dma(pool, Wfi.t[:, j, :], w_fi[l, j * 128:(j + 1) * 128, :], [], [Wfi], Wfi, "ld")
        for f in range(22):
            fw.dma(pool, Wfo.t[:, f, :], w_fo[l, f * 128:(f + 1) * 128, :], [], [Wfo], Wfo, "ld")
        if last:
            fnb = fw.tile("fnb", [128, 1024], F32)
            fw.dma(sp, fnb.t[:], fnw.partition_broadcast(128), [], [fnb], fnb, "ld")
        TT = 256 if nch % 2 == 0 else 128
        nsub = TT // 128
        xts = [fw.tile("fxt%d" % i, [128, 1024], F32) for i in range(2 * nsub)]
        ss = fw.tile("fss", [128, 8], F32)
        t1 = fw.tile("ft1", [128, 1024], F32)
        t2 = fw.tile("ft2", [128, 1024], F32)
        hb = fw.tile("fhb", [128, 1024], BF16)
        hT = fw.tile("fhT", [128, 8, TT], BF16)
        actT = fw.tile("actT", [128, 22, TT], BF16)
        sg = [fw.tile("sg%d" % i, [128, TT], F32) for i in range(2)]
        fdst = "out" if last else dst_name
        for tt in range(S // TT):
            for sub in range(nsub):
                c = tt * nsub + sub
                xt = xts[(tt % 2) * nsub + sub]
                db = dchunk(dst_name, c)
                fw.dma(sp, xt.t[:], dram_ap[dst_name][c * 128:(c + 1) * 128, :], [db], [xt], xt, "ld")
                norm_mod_T(xt, mods, ss, t1, hb, hT, sub * 128, TT)
            for f in range(22):
                bk = PB[2 + (f % 4)]
                for j in range(8):
                    fw.op(pe, lambda e, j=j, f=f, bk=bk: e.matmul(bk.t[:, 0:TT], lhsT=Wfi.t[:, j, f * 128:(f + 1) * 128],
                                                                 rhs=hT.t[:, j, :], start=(j == 0), stop=(j == 7)),
                          [hT, Wfi], [bk], inc=False)
                for j in range(8):
                    fw.op(pe, lambda e, j=j, f=f, bk=bk: e.matmul(bk.t[:, 256:256 + TT],
                                                                 lhsT=Wfi.t[:, j, DFF + f * 128:DFF + (f + 1) * 128],
                                                                 rhs=hT.t[:, j, :], start=(j == 0), stop=(j == 7)),
                          [hT, Wfi], [bk], inc=(j == 7))
                s_ = sg[f % 2]
                fw.op(act, lambda e, bk=bk, s_=s_: e.activation(out=s_.t[:], in_=bk.t[:, 0:TT], func=AF.Tanh, scale=0.5),
                      [bk], [s_])
                fw.op(dve, lambda e, bk=bk, s_=s_: e.scalar_tensor_tensor(out=s_.t[:], in0=s_.t[:], scalar=1.0,
                                                                         in1=bk.t[:, 0:TT], op0=ALU.add, op1=ALU.mult),
                      [s_, bk], [s_])
                fw.op(dve, lambda e, bk=bk, s_=s_, f=f: e.scalar_tensor_tensor(out=actT.t[:, f, :], in0=s_.t[:], scalar=0.5,
                                                                              in1=bk.t[:, 256:256 + TT], op0=ALU.mult,
                                                                              op1=ALU.mult), [s_, bk], [actT])
            for sub in range(nsub):
                c = tt * nsub + sub
                xt = xts[(tt % 2) * nsub + sub]
                for nb in range(2):
                    bk = PB[6 + nb]
                    for f in range(22):
                        fw.op(pe, lambda e, f=f, nb=nb, bk=bk, sub=sub: e.matmul(
                            bk.t[:], lhsT=actT.t[:, f, sub * 128:(sub + 1) * 128], rhs=Wfo.t[:, f, nb * 512:(nb + 1) * 512],
                            start=(f == 0), stop=(f == 21)), [actT, Wfo], [bk], inc=(f == 21))
                    fw.op(dve, lambda e, nb=nb, bk=bk: e.tensor_tensor(t2.t[:, nb * 512:(nb + 1) * 512], bk.t[:],
                                                                        mods.t[:, 2048 + nb * 512:2048 + (nb + 1) * 512],
                                                                        ALU.mult), [bk, mods], [t2])
                fw.op(pool, lambda e, xt=xt: e.tensor_tensor(xt.t[:], xt.t[:], t2.t[:], ALU.add), [xt, t2], [xt])
                if last:
                    fw.op(act, lambda e, xt=xt: e.activation(out=hb.t[:], in_=xt.t[:], func=AF.Square,
                                                             accum_out=ss.t[:, 4:5]), [xt], [hb, ss])
                    fw.op(dve, lambda e: e.tensor_scalar(ss.t[:, 5:6], ss.t[:, 4:5], 1.0 / D, EPS, ALU.mult, ALU.add),
                          [ss], [ss])
                    fw.op(pool, lambda e: e.tensor_tensor(ss.t[:, 6:7], ss.t[:, 5:6], mhalf.t[:, 0:1], ALU.pow),
                          [ss, mhalf], [ss])
                    fw.op(dve, lambda e, xt=xt: e.scalar_tensor_tensor(out=xt.t[:], in0=xt.t[:], scalar=ss.t[:, 6:7],
                                                                       in1=fnb.t[:], op0=ALU.mult, op1=ALU.mult),
                          [xt, ss, fnb], [xt])
                ob = dchunk(fdst, c)
                fw.dma(sp, dram_ap[fdst][c * 128:(c + 1) * 128, :], xt.t[:], [xt], [ob], xt, "st")
        fw.barrier()
        fw.release(mF)
        src_name = dst_name

    fw.barrier()
    fw.replay()
    fw.close()
    stats = {e.name: e.n_instr for e in fw.engs}
    build.last_fw = fw
    return nc, stats


def prep_inputs(inputs, depth=4):
    g = {k: np.ascontiguousarray(np.asarray(v, dtype=np.float32)) for k, v in inputs.items()}
    L = depth
    cqk = np.concatenate([g["conv_qk_w"][:L], g["conv_qk_b"][:L, None, :]], axis=1)
    cqk = np.ascontiguousarray(cqk.reshape(L, 5, 8, 128).transpose(0, 3, 2, 1))
    cssm = np.concatenate([g["conv_ssm_w"][:L], g["conv_ssm_b"][:L, None, :]], axis=1)
    cssm = np.ascontiguousarray(cssm.reshape(L, 5, 12, 128).transpose(0, 3, 2, 1))
    small = np.zeros((L, 64), np.float32)
    small[:, 0:4] = g["i_bias"][:L]
    small[:, 4:8] = g["f_bias"][:L]
    small[:, 8:24] = g["dt_bias"][:L]
    small[:, 24:40] = g["a_log"][:L]
    small[:, 40:56] = g["d_skip"][:L]
    shared = {
        "w_ada": g["w_ada"][:L], "b_ada": g["b_ada"][:L], "norm1_w": g["norm1_w"][:L], "w_in": g["w_in"][:L],
        "cqk": cqk, "small": small, "mlstm_norm_w": g["mlstm_norm_w"][:L], "cssm": cssm,
        "ssm_norm_w": g["ssm_norm_w"][:L], "w_out": g["w_out"][:L], "norm2_w": g["norm2_w"][:L],
        "w_ffn_in": g["w_ffn_in"][:L], "w_ffn_out": g["w_ffn_out"][:L], "final_norm_w": g["final_norm_w"],
    }
    maps = []
    for b in range(g["x"].shape[0]):
        m = dict(shared)
        m["x"] = np.ascontiguousarray(g["x"][b])
        m["c_t"] = np.ascontiguousarray(g["c"][b].reshape(8, 128).T)
        maps.append(m)
    return maps


def kernel(**inputs):
    x = np.asarray(inputs["x"])
    B, S, _ = x.shape
    nc, _ = build(S=S, depth=4)
    maps = prep_inputs(inputs, depth=4)
    res = run_bass_kernel_spmd(nc, maps, core_ids=list(range(B)))
    return np.stack([np.asarray(r["out"], dtype=np.float32) for r in res.results], axis=0)
```
